# Optimizing a Trainium2 kernel written in Bass

```python
import math
import jax, jax.numpy as jnp
from jax import lax
import numpy as np

D_MODEL = 1024
BATCH = 4
SEQ = 8192
DEPTH = 2

N_ATTN_LAYERS = (DEPTH + 1) // 2
N_REC_LAYERS = DEPTH // 2
HEAD_DIM = 64
BLOCK = 128
A_Q_HEADS = 8
A_KV_HEADS = 2
A_WINDOW = 128
B_HEADS = 8
B_BRANCHES = ((128, 1), (512, 4), (2048, 16))
N_ATTN_HEADS = A_Q_HEADS + B_HEADS
ATTN_SPLITS = [A_Q_HEADS * HEAD_DIM, A_KV_HEADS * HEAD_DIM, A_KV_HEADS * HEAD_DIM,
               B_HEADS * HEAD_DIM, B_HEADS * HEAD_DIM, B_HEADS * HEAD_DIM]
ATTN_IN = sum(ATTN_SPLITS)
ATTN_OUT = N_ATTN_HEADS * HEAD_DIM
S5_GROUP = 16
S5_GROUPS = 16
S5_WIDTH = S5_GROUP * S5_GROUPS
S5_STATE = 64
DN_HEADS = 6
DN_DK = 128
DN_DV = 128
DN_CONV = 4
DN_CHUNK = 64
REC_SPLITS = [S5_WIDTH, DN_HEADS * DN_DK, DN_HEADS * DN_DK, DN_HEADS * DN_DV,
              DN_HEADS * DN_DV, DN_HEADS, DN_HEADS]
REC_IN = sum(REC_SPLITS)
REC_OUT = S5_WIDTH + DN_HEADS * DN_DV
D_FF = 2816
FFN_CONV = 3
EPS = 1e-6

kernel_name = "hybrid_swa_dilated_s5_deltanet_block"


def split_cols(t, sizes):
    offs = np.cumsum(sizes)[:-1]
    return jnp.split(t, [int(o) for o in offs], axis=-1)


def rms_norm(x, w):
    xf = x.astype(jnp.float32)
    y = xf * lax.rsqrt(jnp.mean(xf * xf, axis=-1, keepdims=True) + EPS)
    return (y * w.astype(jnp.float32)).astype(x.dtype)


def causal_dwconv(x, w):
    width, ch = w.shape
    xp = jnp.pad(x, ((0, 0), (width - 1, 0), (0, 0)))
    return lax.conv_general_dilated(xp, w[:, None, :].astype(x.dtype), window_strides=(1,),
                                    padding="VALID", dimension_numbers=("NWC", "WIO", "NWC"),
                                    feature_group_count=ch)


def alibi_slopes(n):
    return jnp.asarray(2.0 ** (-8.0 * np.arange(1, n + 1) / n), dtype=jnp.float32)


def banded_attention(q, k, v, slopes, step, max_dist):
    b, L, K, R, hd = q.shape
    nb = L // BLOCK
    qb = q.reshape(b, nb, BLOCK, K, R, hd)
    pad = ((0, 0), (BLOCK, 0), (0, 0), (0, 0))
    kb = jnp.pad(k, pad).reshape(b, nb + 1, BLOCK, K, hd)
    vb = jnp.pad(v, pad).reshape(b, nb + 1, BLOCK, K, hd)
    kw = jnp.concatenate([kb[:, :-1], kb[:, 1:]], axis=2)
    vw = jnp.concatenate([vb[:, :-1], vb[:, 1:]], axis=2)
    s = jnp.einsum("bnqkrd,bnskd->bnkrqs", qb, kw,
                   preferred_element_type=jnp.float32) * (hd ** -0.5)
    dist = BLOCK + jnp.arange(BLOCK)[:, None] - jnp.arange(2 * BLOCK)[None, :]
    after_start = (jnp.arange(nb)[:, None, None] > 0) | (jnp.arange(2 * BLOCK)[None, None, :] >= BLOCK)
    valid = (dist >= 0) & (dist <= max_dist) & after_start
    bias = -slopes.astype(jnp.float32)[:, :, None, None] * (step * dist).astype(jnp.float32)
    s = jnp.where(valid[None, :, None, None], s + bias, -jnp.inf)
    m = jnp.max(s, axis=-1, keepdims=True)
    p = jnp.exp(s - m)
    l = jnp.sum(p, axis=-1)
    o = jnp.einsum("bnkrqs,bnskd->bnqkrd", p.astype(v.dtype), vw,
                   preferred_element_type=jnp.float32)
    o = o / jnp.moveaxis(l, -1, 2)[..., None]
    lse = jnp.moveaxis(m[..., 0] + jnp.log(l), -1, 2)
    return o.reshape(b, L, K, R, hd).astype(q.dtype), lse.reshape(b, L, K, R)


def dilated_branch(q, k, v, slopes, window, dilation):
    b, L, H, hd = q.shape
    span = dilation * BLOCK
    Lp = -(-L // span) * span
    pad = ((0, 0), (0, Lp - L), (0, 0), (0, 0))

    def strided(t):
        return jnp.pad(t, pad).reshape(b, Lp // dilation, dilation * H, hd)

    o, lse = banded_attention(strided(q)[:, :, :, None, :], strided(k), strided(v),
                              jnp.tile(slopes, dilation)[:, None], dilation, window // dilation)
    o = o[:, :, :, 0].reshape(b, Lp, H, hd)[:, :L]
    lse = lse[..., 0].reshape(b, Lp, H)[:, :L]
    return o, lse


def attention_mixer(h, w_in, q_norm_a, k_norm_a, q_norm_b, k_norm_b, sinks, w_out):
    b, L, _ = h.shape
    rep = A_Q_HEADS // A_KV_HEADS
    qa, ka, va, qb, kb, vb = split_cols(h @ w_in, ATTN_SPLITS)
    slopes = alibi_slopes(N_ATTN_HEADS)
    qa = rms_norm(qa.reshape(b, L, A_KV_HEADS, rep, HEAD_DIM), q_norm_a)
    ka = rms_norm(ka.reshape(b, L, A_KV_HEADS, HEAD_DIM), k_norm_a)
    va = va.reshape(b, L, A_KV_HEADS, HEAD_DIM)
    oa, lse_a = banded_attention(qa, ka, va, slopes[:A_Q_HEADS].reshape(A_KV_HEADS, rep),
                                 1, A_WINDOW - 1)
    keep = jax.nn.sigmoid(lse_a - sinks.astype(jnp.float32).reshape(A_KV_HEADS, rep))
    oa = (oa.astype(jnp.float32) * keep[..., None]).reshape(b, L, A_Q_HEADS * HEAD_DIM)
    qb = rms_norm(qb.reshape(b, L, B_HEADS, HEAD_DIM), q_norm_b)
    kb = rms_norm(kb.reshape(b, L, B_HEADS, HEAD_DIM), k_norm_b)
    vb = vb.reshape(b, L, B_HEADS, HEAD_DIM)
    outs, lses = [], []
    for window, dilation in B_BRANCHES:
        o, l = dilated_branch(qb, kb, vb, slopes[A_Q_HEADS:], window, dilation)
        outs.append(o)
        lses.append(l)
    wts = jax.nn.softmax(jnp.stack(lses), axis=0)
    ob = jnp.einsum("gblh,gblhd->blhd", wts, jnp.stack(outs).astype(jnp.float32))
    ob = ob.reshape(b, L, B_HEADS * HEAD_DIM)
    return jnp.concatenate([oa, ob], axis=-1).astype(h.dtype) @ w_out


def s5_mixer(u, lam_re, lam_im, log_dt, b_re, b_im, c_re, c_im, d_skip, glu_w, glu_b):
    f32 = jnp.float32
    bsz, L, _ = u.shape
    uf = u.astype(f32).reshape(bsz, L, S5_GROUPS, S5_GROUP)
    lr, li = lam_re.astype(f32), lam_im.astype(f32)
    dt = jnp.exp(log_dt.astype(f32))[:, None]
    mag, ang = jnp.exp(lr * dt), li * dt
    ab_re, ab_im = mag * jnp.cos(ang), mag * jnp.sin(ang)
    nr, ni = ab_re - 1.0, ab_im
    den = lr * lr + li * li
    f_re = (nr * lr + ni * li) / den
    f_im = (ni * lr - nr * li) / den
    bu_re = jnp.einsum("blgi,gpi->blgp", uf, b_re.astype(f32))
    bu_im = jnp.einsum("blgi,gpi->blgp", uf, b_im.astype(f32))
    e_re = f_re * bu_re - f_im * bu_im
    e_im = f_re * bu_im + f_im * bu_re
    a_re = jnp.broadcast_to(ab_re, e_re.shape)
    a_im = jnp.broadcast_to(ab_im, e_im.shape)

    def combine(e1, e2):
        a1r, a1i, b1r, b1i = e1
        a2r, a2i, b2r, b2i = e2
        return (a2r * a1r - a2i * a1i, a2r * a1i + a2i * a1r,
                a2r * b1r - a2i * b1i + b2r, a2r * b1i + a2i * b1r + b2i)

    _, _, x_re, x_im = lax.associative_scan(combine, (a_re, a_im, e_re, e_im), axis=1)
    y = (jnp.einsum("blgp,gip->blgi", x_re, c_re.astype(f32))
         - jnp.einsum("blgp,gip->blgi", x_im, c_im.astype(f32))
         + d_skip.astype(f32).reshape(S5_GROUPS, S5_GROUP) * uf)
    g = jax.nn.gelu(y.reshape(bsz, L, S5_WIDTH))
    return (g * jax.nn.sigmoid(g @ glu_w.astype(f32) + glu_b.astype(f32))).astype(u.dtype)


def chunk_gated_delta_rule(q, k, v, g, beta):
    b, L, H, dk = q.shape
    dv = v.shape[-1]
    n, C = L // DN_CHUNK, DN_CHUNK

    def chunks(t):
        return jnp.moveaxis(t.reshape((b, n, C) + t.shape[2:]), 3, 2)

    q, k, v, g, beta = chunks(q), chunks(k), chunks(v), chunks(g), chunks(beta)
    G = jnp.cumsum(g, axis=-1)
    causal = jnp.tril(jnp.ones((C, C), bool))
    strict = jnp.tril(jnp.ones((C, C), bool), -1)
    diff = G[..., :, None] - G[..., None, :]
    gamma = jnp.where(causal, jnp.exp(jnp.where(causal, diff, 0.0)), 0.0)
    kk = jnp.einsum("bnhid,bnhjd->bnhij", k, k)
    n_mat = jnp.where(strict, beta[..., :, None] * kk * gamma, 0.0)
    rhs = jnp.concatenate([v * beta[..., None], k * (beta * jnp.exp(G))[..., None]], axis=-1)
    sol = lax.linalg.triangular_solve(n_mat + jnp.eye(C, dtype=jnp.float32), rhs,
                                      left_side=True, lower=True, unit_diagonal=True)
    u, w = sol[..., :dv], sol[..., dv:]
    qk = jnp.einsum("bnhid,bnhjd->bnhij", q, k) * gamma
    q_dec = q * jnp.exp(G)[..., None]
    k_dec = k * jnp.exp(G[..., -1:] - G)[..., None]
    g_last = jnp.exp(G[..., -1])

    def step(S, xs):
        u_c, w_c, qk_c, qd_c, kd_c, gl_c = xs
        v_new = u_c - jnp.einsum("bhcd,bhde->bhce", w_c, S)
        o = jnp.einsum("bhcd,bhde->bhce", qd_c, S) + jnp.einsum("bhij,bhje->bhie", qk_c, v_new)
        S = S * gl_c[..., None, None] + jnp.einsum("bhcd,bhce->bhde", kd_c, v_new)
        return S, o

    xs = tuple(jnp.moveaxis(t, 1, 0) for t in (u, w, qk, q_dec, k_dec, g_last))
    _, o = lax.scan(step, jnp.zeros((b, H, dk, dv), jnp.float32), xs)
    return jnp.moveaxis(jnp.moveaxis(o, 0, 1), 2, 3).reshape(b, L, H, dv)


def gated_deltanet_mixer(q, k, v, z, a, beta_raw, conv_w, a_log, dt_bias, out_norm):
    f32 = jnp.float32
    bsz, L, _ = q.shape
    qkv = jax.nn.silu(causal_dwconv(jnp.concatenate([q, k, v], axis=-1), conv_w)).astype(f32)
    q, k, v = split_cols(qkv, [DN_HEADS * DN_DK, DN_HEADS * DN_DK, DN_HEADS * DN_DV])
    q = q.reshape(bsz, L, DN_HEADS, DN_DK)
    k = k.reshape(bsz, L, DN_HEADS, DN_DK)
    v = v.reshape(bsz, L, DN_HEADS, DN_DV)
    q = q * lax.rsqrt(jnp.sum(q * q, axis=-1, keepdims=True) + EPS) * (DN_DK ** -0.5)
    k = k * lax.rsqrt(jnp.sum(k * k, axis=-1, keepdims=True) + EPS)
    beta = jax.nn.sigmoid(beta_raw.astype(f32))
    g = -jnp.exp(a_log.astype(f32)) * jax.nn.softplus(a.astype(f32) + dt_bias.astype(f32))
    o = chunk_gated_delta_rule(q, k, v, g, beta)
    o = rms_norm(o, out_norm) * jax.nn.silu(z.astype(f32).reshape(bsz, L, DN_HEADS, DN_DV))
    return o.reshape(bsz, L, DN_HEADS * DN_DV)


def recurrent_mixer(h, w_in, lam_re, lam_im, log_dt, b_re, b_im, c_re, c_im, d_skip, glu_w,
                    glu_b, dn_conv, a_log, dt_bias, out_norm, w_out):
    u, q, k, v, z, a, beta_raw = split_cols(h @ w_in, REC_SPLITS)
    yc = s5_mixer(u, lam_re, lam_im, log_dt, b_re, b_im, c_re, c_im, d_skip, glu_w, glu_b)
    yd = gated_deltanet_mixer(q, k, v, z, a, beta_raw, dn_conv, a_log, dt_bias, out_norm)
    return jnp.concatenate([yc.astype(h.dtype), yd.astype(h.dtype)], axis=-1) @ w_out


def conv_ffn(h, w_up, conv_w, w_down):
    up = causal_dwconv(h @ w_up, conv_w)
    a, b = jnp.split(up, 2, axis=-1)
    return (jax.nn.silu(a) * b) @ w_down


def modulate(x, norm_w, shift, scale):
    return rms_norm(x, norm_w) * (1.0 + scale[:, None, :]) + shift[:, None, :]


def setup_inputs(seed: int = 0) -> dict:
    key = jax.random.key(seed)
    ks = iter(jax.random.split(key, 40))
    f32 = jnp.float32
    na, nr = N_ATTN_LAYERS, N_REC_LAYERS

    def nrm(shape, scale):
        return jax.random.normal(next(ks), shape, f32) * scale

    def gain(shape):
        return 1.0 + nrm(shape, 0.02)

    x = nrm((BATCH, SEQ, D_MODEL), 1.0)
    c = nrm((BATCH, D_MODEL), 1.0)
    ada_w = nrm((DEPTH, D_MODEL, 6 * D_MODEL), 0.5 * D_MODEL ** -0.5)
    ada_b = nrm((DEPTH, 6 * D_MODEL), 0.02)
    norm_mix = gain((DEPTH, D_MODEL))
    norm_ffn = gain((DEPTH, D_MODEL))
    attn_w_in = nrm((na, D_MODEL, ATTN_IN), D_MODEL ** -0.5)
    attn_q_norm_a = gain((na, HEAD_DIM))
    attn_k_norm_a = gain((na, HEAD_DIM))
    attn_q_norm_b = gain((na, HEAD_DIM))
    attn_k_norm_b = gain((na, HEAD_DIM))
    attn_sinks = nrm((na, A_Q_HEADS), 1.0)
    attn_w_out = nrm((na, ATTN_OUT, D_MODEL), ATTN_OUT ** -0.5)
    rec_w_in = nrm((nr, D_MODEL, REC_IN), D_MODEL ** -0.5)
    s5_lambda_re = -0.5 + nrm((nr, S5_GROUPS, S5_STATE), 0.01)
    s5_lambda_im = jnp.pi * jnp.arange(S5_STATE, dtype=f32) + nrm((nr, S5_GROUPS, S5_STATE), 0.01)
    s5_log_dt = jax.random.uniform(next(ks), (nr, S5_GROUPS), f32, math.log(1e-3), math.log(1e-1))
    s5_b_re = nrm((nr, S5_GROUPS, S5_STATE, S5_GROUP), (2 * S5_GROUP) ** -0.5)
    s5_b_im = nrm((nr, S5_GROUPS, S5_STATE, S5_GROUP), (2 * S5_GROUP) ** -0.5)
    s5_c_re = nrm((nr, S5_GROUPS, S5_GROUP, S5_STATE), S5_STATE ** -0.5)
    s5_c_im = nrm((nr, S5_GROUPS, S5_GROUP, S5_STATE), S5_STATE ** -0.5)
    s5_d = nrm((nr, S5_WIDTH), 1.0)
    s5_glu_w = nrm((nr, S5_WIDTH, S5_WIDTH), S5_WIDTH ** -0.5)
    s5_glu_b = nrm((nr, S5_WIDTH), 0.02)
    dn_conv = nrm((nr, DN_CONV, DN_HEADS * (2 * DN_DK + DN_DV)), DN_CONV ** -0.5)
    dn_a_log = jnp.log(jax.random.uniform(next(ks), (nr, DN_HEADS), f32, 1.0, 16.0))
    dt0 = jnp.exp(jax.random.uniform(next(ks), (nr, DN_HEADS), f32, math.log(1e-3), math.log(1e-1)))
    dn_dt_bias = dt0 + jnp.log(-jnp.expm1(-dt0))
    dn_out_norm = gain((nr, DN_DV))
    rec_w_out = nrm((nr, REC_OUT, D_MODEL), REC_OUT ** -0.5)
    ffn_w_up = nrm((DEPTH, D_MODEL, 2 * D_FF), D_MODEL ** -0.5)
    ffn_conv = nrm((DEPTH, FFN_CONV, 2 * D_FF), FFN_CONV ** -0.5)
    ffn_w_down = nrm((DEPTH, D_FF, D_MODEL), D_FF ** -0.5)
    return {"x": x, "c": c, "ada_w": ada_w, "ada_b": ada_b, "norm_mix": norm_mix,
            "norm_ffn": norm_ffn, "attn_w_in": attn_w_in, "attn_q_norm_a": attn_q_norm_a,
            "attn_k_norm_a": attn_k_norm_a, "attn_q_norm_b": attn_q_norm_b,
            "attn_k_norm_b": attn_k_norm_b, "attn_sinks": attn_sinks, "attn_w_out": attn_w_out,
            "rec_w_in": rec_w_in, "s5_lambda_re": s5_lambda_re, "s5_lambda_im": s5_lambda_im,
            "s5_log_dt": s5_log_dt, "s5_b_re": s5_b_re, "s5_b_im": s5_b_im, "s5_c_re": s5_c_re,
            "s5_c_im": s5_c_im, "s5_d": s5_d, "s5_glu_w": s5_glu_w, "s5_glu_b": s5_glu_b,
            "dn_conv": dn_conv, "dn_a_log": dn_a_log, "dn_dt_bias": dn_dt_bias,
            "dn_out_norm": dn_out_norm, "rec_w_out": rec_w_out, "ffn_w_up": ffn_w_up,
            "ffn_conv": ffn_conv, "ffn_w_down": ffn_w_down}


def reference(x, c, ada_w, ada_b, norm_mix, norm_ffn, attn_w_in, attn_q_norm_a, attn_k_norm_a,
              attn_q_norm_b, attn_k_norm_b, attn_sinks, attn_w_out, rec_w_in, s5_lambda_re,
              s5_lambda_im, s5_log_dt, s5_b_re, s5_b_im, s5_c_re, s5_c_im, s5_d, s5_glu_w,
              s5_glu_b, dn_conv, dn_a_log, dn_dt_bias, dn_out_norm, rec_w_out, ffn_w_up,
              ffn_conv, ffn_w_down):
    cond = jax.nn.silu(c)
    for layer in range(DEPTH):
        mod = cond @ ada_w[layer] + ada_b[layer]
        sh1, sc1, g1, sh2, sc2, g2 = jnp.split(mod, 6, axis=-1)
        h = modulate(x, norm_mix[layer], sh1, sc1)
        i = layer // 2
        if layer % 2 == 0:
            y = attention_mixer(h, attn_w_in[i], attn_q_norm_a[i], attn_k_norm_a[i],
                                attn_q_norm_b[i], attn_k_norm_b[i], attn_sinks[i], attn_w_out[i])
        else:
            y = recurrent_mixer(h, rec_w_in[i], s5_lambda_re[i], s5_lambda_im[i], s5_log_dt[i],
                                s5_b_re[i], s5_b_im[i], s5_c_re[i], s5_c_im[i], s5_d[i],
                                s5_glu_w[i], s5_glu_b[i], dn_conv[i], dn_a_log[i], dn_dt_bias[i],
                                dn_out_norm[i], rec_w_out[i])
        x = x + g1[:, None, :] * y
        h = modulate(x, norm_ffn[layer], sh2, sc2)
        x = x + g2[:, None, :] * conv_ffn(h, ffn_w_up[layer], ffn_conv[layer], ffn_w_down[layer])
    return x
```

```python
import os, math
import numpy as np
from contextlib import ExitStack
import concourse.bass as bass
import concourse.mybir as mybir
from concourse.bass_utils import run_bass_kernel_spmd

F32 = mybir.dt.float32
BF16 = mybir.dt.bfloat16
ALU = mybir.AluOpType
AF = mybir.ActivationFunctionType


class KB:
    def __init__(self, nc, es):
        self.nc, self.es = nc, es
        self.sem_es = es
        self.engs = {"pe": nc.tensor, "act": nc.scalar, "dve": nc.vector,
                     "pool": nc.gpsimd, "sp": nc.sync}
        self.sems = {}
        self.cnt = {}
        for e in ["pe", "act", "dve", "pool"]:
            self.sems[e] = es.enter_context(nc.semaphore("s_" + e))
            self.cnt[e] = 0
        self.known = {e: {} for e in self.engs}
        self.last_w = {}
        self.readers = {}
        self.n_inst = 0

    def sb(self, name, shape, dt=F32):
        return self.es.enter_context(self.nc.sbuf_tensor(name, list(shape), dt))

    def ps(self, name, shape, dt=F32):
        return self.es.enter_context(self.nc.psum_tensor(name, list(shape), dt))

    def dram(self, name, shape, dt, kind="Internal"):
        return self.nc.dram_tensor(name, list(shape), dt, kind=kind)

    @staticmethod
    def _key(x):
        if isinstance(x, tuple):
            return x[1]
        return x.tensor.name

    @staticmethod
    def _ap(x):
        return x[0] if isinstance(x, tuple) else x

    def _wait(self, eng, tok, war=False):
        s, v = tok
        if s == eng and eng == "pe":
            return
        kn = self.known[eng]
        if kn.get(s, 0) >= v:
            return
        self.engs[eng].wait_ge(self.sems[s], v)
        kn[s] = v

    def _deps(self, eng, rkeys, wkeys):
        for k in rkeys:
            t = self.last_w.get(k)
            if t is not None:
                self._wait(eng, t)
            if isinstance(k, str) and k.startswith("bk"):
                rd = self.readers.get(k)
                if rd:
                    for s, v in list(rd.items()):
                        if s != eng:
                            self._wait(eng, (s, v))
        for k in wkeys:
            t = self.last_w.get(k)
            if t is not None:
                self._wait(eng, t)
            rd = self.readers.get(k)
            if rd:
                for s, v in rd.items():
                    self._wait(eng, (s, v), war=True)

    def _commit(self, tok, rkeys, wkeys):
        s, v = tok
        for k in rkeys:
            rd = self.readers.setdefault(k, {})
            if rd.get(s, 0) < v:
                rd[s] = v
        for k in wkeys:
            self.last_w[k] = tok
            self.readers[k] = {}

    def op(self, eng, fn, reads, writes):
        rkeys = [self._key(r) for r in reads]
        wkeys = [self._key(w) for w in writes]
        self._deps(eng, rkeys, wkeys)
        ins = fn()
        ins.then_inc(self.sems[eng], 1)
        self.cnt[eng] += 1
        self.n_inst += 1
        self._commit((eng, self.cnt[eng]), rkeys, wkeys)
        return ins

    def dma(self, pairs, queue="sp"):
        rkeys = [self._key(i) for _, i in pairs]
        wkeys = [self._key(o) for o, _ in pairs]
        self._deps(queue, rkeys, wkeys)
        sname = "d_" + str(wkeys[0])
        if sname not in self.sems:
            self.sems[sname] = self.sem_es.enter_context(self.nc.semaphore(sname[:40].replace(" ", "")))
            self.cnt[sname] = 0
        for o, i in pairs:
            self.engs[queue].dma_start(out=self._ap(o), in_=self._ap(i)).then_inc(self.sems[sname], 16)
            self.cnt[sname] += 16
            self.n_inst += 1
        self._commit((sname, self.cnt[sname]), rkeys, wkeys)

    def barrier(self):
        for eng in self.engs:
            for s_, v in self.cnt.items():
                if v > 0:
                    self._wait(eng, (s_, v), war=False) if s_ != eng else None

    def finish(self, keys, eng="pool"):
        for k in keys:
            t = self.last_w.get(k)
            if t is not None:
                self._wait(eng, t)

    def mm(self, out, lhsT, rhs, start=True, stop=True, **kw):
        o, l, r = self._ap(out), self._ap(lhsT), self._ap(rhs)
        return self.op("pe", lambda: self.nc.tensor.matmul(o, l, r, start=start, stop=stop, **kw),
                       [lhsT, rhs], [out])

    def tr(self, out, in_, ident):
        o, i, d = self._ap(out), self._ap(in_), self._ap(ident)
        return self.op("pe", lambda: self.nc.tensor.transpose(o, i, d), [in_, ident], [out])

    def act(self, out, in_, func, bias=None, scale=None, extra_reads=(), accum_out=None):
        o, i = self._ap(out), self._ap(in_)
        kw = {}
        rd = [in_] + list(extra_reads)
        wr = [out]
        if accum_out is not None:
            kw["accum_out"] = self._ap(accum_out)
            wr.append(accum_out)
        if bias is not None:
            if isinstance(bias, (int, float)):
                kw["bias"] = float(bias)
            else:
                kw["bias"] = self._ap(bias)
                rd.append(bias)
        if scale is not None:
            if isinstance(scale, (int, float)):
                kw["scale"] = float(scale)
            else:
                kw["scale"] = self._ap(scale)
                rd.append(scale)
        return self.op("act", lambda: self.nc.scalar.activation(o, i, func, **kw), rd, wr)

    def tt(self, out, in0, in1, op, eng="dve"):
        o, a, b = self._ap(out), self._ap(in0), self._ap(in1)
        return self.op(eng, lambda: self.engs[eng].tensor_tensor(o, a, b, op), [in0, in1], [out])

    def ts(self, out, in0, s1, op0, s2=None, op1=None, eng="dve"):
        o, a = self._ap(out), self._ap(in0)
        rd = [in0]
        v1 = s1
        if not isinstance(s1, (int, float)):
            v1 = self._ap(s1)
            rd.append(s1)
        v2 = s2
        if s2 is not None and not isinstance(s2, (int, float)):
            v2 = self._ap(s2)
            rd.append(s2)
        if op1 is None:
            return self.op(eng, lambda: self.engs[eng].tensor_scalar(o, a, v1, None, op0), rd, [out])
        return self.op(eng, lambda: self.engs[eng].tensor_scalar(o, a, v1, v2, op0, op1), rd, [out])

    def stt(self, out, in0, s, in1, op0, op1):
        o, a, b = self._ap(out), self._ap(in0), self._ap(in1)
        rd = [in0, in1]
        v = s
        if not isinstance(s, (int, float)):
            v = self._ap(s)
            rd.append(s)
        return self.op("dve", lambda: self.nc.vector.scalar_tensor_tensor(o, a, v, b, op0, op1), rd, [out])

    def copy(self, out, in_, eng="dve"):
        o, i = self._ap(out), self._ap(in_)
        if eng == "act":
            return self.op("act", lambda: self.nc.scalar.copy(o, i), [in_], [out])
        return self.op(eng, lambda: self.engs[eng].tensor_copy(o, i), [in_], [out])

    def memset(self, out, val, eng="pool"):
        o = self._ap(out)
        return self.op(eng, lambda: self.engs[eng].memset(o, val), [], [out])

    def recip(self, out, in_):
        o, i = self._ap(out), self._ap(in_)
        return self.op("dve", lambda: self.nc.vector.reciprocal(o, i), [in_], [out])

    def scan(self, out, d0, d1, init, op0=ALU.mult, op1=ALU.add):
        o, a, b = self._ap(out), self._ap(d0), self._ap(d1)
        rd = [d0, d1]
        v = init
        if not isinstance(init, (int, float)):
            v = self._ap(init)
            rd.append(init)
        return self.op("dve", lambda: self.nc.vector.tensor_tensor_scan(o, a, b, v, op0, op1), rd, [out])


import ml_dtypes, os
DBG = int(os.environ.get('M0_DBG', '9'))

SEQ = 8192
SPAN = 2048
NSPAN = SEQ // SPAN
PT = 256
EPS = 1e-6
BIG = 1.0e9


def emit_mod(kb, nc, adaw, adab_sb, silc, ncols_chunks, psm, modT, wring):
    for j in range(ncols_chunks):
        w = wring[j % len(wring)]
        kb.dma([(w, adaw[:, j * 128:(j + 1) * 128].rearrange("(c p) n -> p c n", p=128))])
        for k in range(8):
            kb.mm(psm[:, j:j + 1], w[:, k, :], silc[:, k:k + 1], start=(k == 0), stop=(k == 7))
    kb.tt(modT[:, 0:ncols_chunks], psm[:, 0:ncols_chunks], adab_sb[:, 0:ncols_chunks], ALU.add)


def build_m0(nspan=NSPAN, do_attn=True, do_proj=True):
    nc = bass.Bass("TRN2", target_bir_lowering=False)
    es = ExitStack()
    with es:
        kb = KB(nc, es)
        din = lambda n, s, dt=F32: nc.dram_tensor(n, list(s), dt, kind="ExternalInput").ap()
        xT = din("xT", [1024, SEQ])
        ccol = din("ccol", [128, 8])
        adaw = din("adaw", [1024, 2048])
        adab = din("adab", [128, 16])
        nmix = din("nmix", [128, 8])
        win = din("win", [1024, 1280])
        nrm = din("nrm", [128, 4])
        sinks = din("sinks", [128, 2])
        dist = din("dist", [128, 4, 128])
        nslope = din("nslope", [128, 16])
        ident = din("ident", [128, 128])
        bones = din("bones", [128, 128])
        oT = nc.dram_tensor("oT", [512, SEQ], BF16, kind="ExternalOutput").ap()

        banks = [kb.ps(f"bk{i}", [128, 512]) for i in range(7)]
        bkT = kb.ps("bkT", [128, 1024], BF16)
        small = kb.sb("small", [128, 64])
        c_sb, adab_sb, nmix_sb = small[:, 0:8], small[:, 8:24], small[:, 24:32]
        kb.dma([(small[:, 0:8], ccol), (small[:, 8:24], adab), (small[:, 24:32], nmix)])
        small2 = kb.sb("small2", [128, 64])
        kb.dma([(small2[:, 0:4], nrm), (small2[:, 4:6], sinks), (small2[:, 8:24], nslope)])
        nrm_sb, sink_sb, nsl_sb = small2[:, 0:4], small2[:, 4:6], small2[:, 8:24]
        cf = kb.sb("cf", [128, 3, 128])
        kb.dma([(cf[:, 0, :], ident), (cf[:, 1, :], bones)])
        kb.memset(cf[:, 2, :], 1.0)
        cb = kb.sb("cb", [128, 3, 128], BF16)
        kb.copy(cb[:], cf[:])
        identb, bonesb, onesb = cb[:, 0, :], cb[:, 1, :], cb[:, 2, :]
        dist_sb = kb.sb("dist_sb", [128, 4, 128])
        kb.dma([(dist_sb[:], dist)])
        MA = kb.sb("MA", [128, 2, 4, 128])
        MB = kb.sb("MB", [128, 3, 2, 4, 128])
        for j in range(4):
            pj = (j % 2) * 2 + j // 2
            for role in range(2):
                kb.act(MA[:, role, pj, :], dist_sb[:, role, :], AF.Exp, scale=nsl_sb[:, j:j + 1])
                for di in range(3):
                    kb.act(MB[:, di, role, pj, :], dist_sb[:, 2 + role, :], AF.Exp,
                           scale=nsl_sb[:, 4 + di * 4 + j:5 + di * 4 + j])
        silc = kb.sb("silc", [128, 8])
        kb.act(silc[:], c_sb, AF.Silu)
        xts = [kb.sb(f"xt{i}", [128, 8, PT]) for i in range(2)]
        wr = [xts[i][:, :, 0:128] for i in range(2)]
        modT = kb.sb("modT", [128, 16])
        emit_mod(kb, nc, adaw, adab_sb, silc, 16, banks[6], modT, wr)
        A1 = kb.sb("A1", [128, 8])
        kb.stt(A1[:], modT[:, 8:16], 1.0, nmix_sb, ALU.add, ALU.mult)
        B1 = modT[:, 0:8]
        nw = kb.sb("nw", [128, 4])
        kb.copy(nw[:], nrm_sb)
        kb.ts(nw[:, 0:1], nrm_sb[:, 0:1], 0.125, ALU.mult)
        kb.ts(nw[:, 2:3], nrm_sb[:, 2:3], 0.125, ALU.mult)
        esink = kb.sb("esink", [128, 2])
        kb.act(esink[:], sink_sb, AF.Exp)
        wb = kb.sb("wb", [128, 8, 1280], BF16)
        wst = wr
        for oc in range(10):
            w = wst[oc % 2]
            kb.dma([(w, win[:, oc * 128:(oc + 1) * 128].rearrange("(c p) n -> p c n", p=128))])
            kb.copy(wb[:, :, oc * 128:(oc + 1) * 128], w, eng="pool")

        hTs = [kb.sb(f"hT{i}", [128, 8, PT], BF16) for i in range(2)]
        sqs = [kb.sb(f"sq{i}", [128, PT], BF16) for i in range(3)]
        rts = [kb.sb(f"rt{i}", [128, PT]) for i in range(3)]
        QT = kb.sb("QT", [128, 4, SPAN], BF16)
        KT = kb.sb("KT", [128, 3, 2 * SPAN], BF16)
        VT = kb.sb("VT", [128, 3, SPAN], BF16)
        Vt = {16: kb.sb("Vt16", [128, 32, 256], BF16), 4: kb.sb("Vt4", [128, 8, 256], BF16),
              1: kb.sb("Vt1", [128, 2, 256], BF16)}
        VA = kb.sb("VA", [128, 2, 128], BF16)
        OB = kb.sb("OB", [128, 2, SPAN])
        LB = kb.sb("LB", [128, 2, SPAN])
        Oout = [kb.sb(f"Oout{i}", [128, 2, SPAN], BF16) for i in range(2)]
        es_ = [kb.sb(f"ex{i}", [128, 512]) for i in range(3)]
        pts = [kb.sb(f"pt{i}", [128, 512], BF16) for i in range(3)]
        dn = [kb.sb(f"dn{i}", [128, 128]) for i in range(2)]
        cnt = {"sq": 0, "rt": 0, "pp": 0, "ex": 0, "S": 0, "O": 0, "L": 0, "T": 0, "dn": 0}

        def nxt(name, ring):
            v = ring[cnt[name] % len(ring)]
            cnt[name] += 1
            return v

        projb = banks[0:3]
        Sb = [(banks[0], banks[1]), (banks[2], banks[3])]
        nb = banks[3:5]
        Ob = banks[4:6]
        Lb = [banks[6]]
        HORD = [0, 2, 1, 3]

        def rstd_from(ps_ap, scale, n):
            r = nxt("rt", rts)
            kb.act(r[:, 0:n], ps_ap, AF.Sqrt, bias=EPS_AP, scale=scale)
            kb.recip(r[:, 0:n], r[:, 0:n])
            return r

        epsc = kb.sb("epsc", [128, 1])
        kb.memset(epsc[:], EPS)
        EPS_AP = epsc[:, 0:1]

        def proj_tile(n, ti):
            t0 = n * SPAN + ti * PT
            lo = ti * PT
            xt = xts[ti % 2]
            kb.dma([(xt[:], xT[:, t0:t0 + PT].rearrange("(c p) t -> p c t", p=128))])
            pss = nb[0]
            for c in range(8):
                sq = nxt("sq", sqs)
                kb.act(sq[:], xt[:, c, :], AF.Square)
                kb.mm(pss[:, 0:PT], onesb, sq[:], start=(c == 0), stop=(c == 7))
            r = rstd_from(pss[:, 0:PT], 1.0 / 1024.0, PT)
            kb.tt(xt[:], xt[:], r[:, 0:PT].unsqueeze(1).to_broadcast([128, 8, PT]), ALU.mult)
            hT = hTs[ti % 2]
            for c in range(8):
                kb.act(hT[:, c, :], xt[:, c, :], AF.Identity, bias=B1[:, c:c + 1], scale=A1[:, c:c + 1])
            kcol = (n % 2) * SPAN + lo
            for oc in range(10):
                ps = nxt("pp", projb)
                for k in range(8):
                    kb.mm(ps[:, 0:PT], wb[:, k, oc * 128:(oc + 1) * 128], hT[:, k, :], start=(k == 0), stop=(k == 7))
                if oc in (3, 8, 9):
                    vi = {3: 0, 8: 1, 9: 2}[oc]
                    kb.copy(VT[:, vi, lo:lo + PT], ps[:, 0:PT], eng="act")
                    continue
                sq = nxt("sq", sqs)
                kb.act(sq[:], ps[:, 0:PT], AF.Square)
                ps2 = nb[1]
                kb.mm(ps2[:, 0:PT], bonesb, sq[:])
                r = rstd_from(ps2[:, 0:PT], 1.0 / 64.0, PT)
                if oc in (0, 1):
                    dst, w = QT[:, oc, lo:lo + PT], nw[:, 0:1]
                elif oc == 2:
                    dst, w = KT[:, 0, kcol:kcol + PT], nw[:, 1:2]
                elif oc in (4, 5):
                    dst, w = QT[:, oc - 2, lo:lo + PT], nw[:, 2:3]
                else:
                    dst, w = KT[:, oc - 5, kcol:kcol + PT], nw[:, 3:4]
                kb.stt(dst, ps[:, 0:PT], w, r[:, 0:PT], ALU.mult, ALU.mult)

        def group(n, d, b, r, first_branch):
            tok0 = 128 * b * d + r
            qs = tok0 - n * SPAN
            qsl = slice(qs, qs + 127 * d + 1, d)

            def kslice(bb):
                t = 128 * bb * d + r
                nn = t // SPAN
                c0 = (nn % 2) * SPAN + (t - nn * SPAN)
                return slice(c0, c0 + 127 * d + 1, d)

            roles = ([("prev", b - 1)] if b > 0 else []) + [("cur", b)]
            slot = (b % 2) * d + r
            vkey = f"Vt{d}_{slot}"
            pT = bkT
            for ci in range(2):
                kb.tr(pT[:, ci * 128:(ci + 1) * 128], VT[:, 1 + ci, qsl], identb)
            kb.copy((Vt[d][:, slot, :], vkey), pT[:, 0:256], eng="act")
            if d == 1:
                kb.tr(pT[:, 256:384], VT[:, 0, qsl], identb)
                kb.copy((VA[:, b % 2, :], f"VA_{b % 2}"), pT[:, 256:384], eng="act")
            if DBG < 2:
                return
            sets = [("B", 0)] + ([("A", 0)] if d == 1 else [])
            di = {1: 0, 4: 1, 16: 2}[d]
            for kind, _ in sets:
                psO = nxt("O", Ob)
                psL = nxt("L", Lb)
                for ri, (role, bb) in enumerate(roles):
                    ksl = kslice(bb)
                    psXY = nxt("S", Sb)
                    for j in range(4):
                        rows = slice((j % 2) * 64, (j % 2) * 64 + 64)
                        psS = psXY[j % 2]
                        cs = slice((j // 2) * 128, (j // 2) * 128 + 128)
                        if kind == "B":
                            kb.mm(psS[:, cs], KT[rows, 1 + j // 2, ksl], QT[rows, 2 + j // 2, qsl])
                        else:
                            kb.mm(psS[:, cs], KT[rows, 0, ksl], QT[rows, j // 2, qsl])
                    e = nxt("ex", es_)
                    kb.act(e[:, 0:256], psXY[0][:, 0:256], AF.Exp)
                    kb.act(e[:, 256:512], psXY[1][:, 0:256], AF.Exp)
                    if DBG < 3:
                        continue
                    pt = pts[(cnt["ex"] - 1) % 3]
                    rI = 0 if role == "cur" else 1
                    msk = MB[:, di, rI, :, :] if kind == "B" else MA[:, rI, :, :]
                    kb.tt(pt[:].rearrange("p (h q) -> p h q", h=4), e[:].rearrange("p (h q) -> p h q", h=4),
                          msk, ALU.mult, eng="pool")
                    if DBG < 4:
                        continue
                    st = (ri == 0)
                    sp = (ri == len(roles) - 1)
                    kslot = (bb % 2) * d + r
                    if kind == "B":
                        for p in range(4):
                            j = HORD[p]
                            kb.mm(psO[:, p * 128:(p + 1) * 128],
                                  (Vt[d][:, kslot, (j // 2) * 128:(j // 2 + 1) * 128], f"Vt{d}_{kslot}"),
                                  pt[:, p * 128:(p + 1) * 128], start=(st and p == 0), stop=(sp and p == 3),
                                  skip_group_check=True)
                    else:
                        kb.mm(psO[:, 0:512], (VA[:, bb % 2, :], f"VA_{bb % 2}"), pt[:, 0:512], start=st, stop=sp)
                    kb.mm(psL[:, 0:512], onesb, pt[:, 0:512], start=st, stop=sp)
                if DBG < 5:
                    continue
                pO = psO[:].rearrange("p (h q) -> p h q", h=4)
                pL = psL[:].rearrange("p (h q) -> p h q", h=4)
                if kind == "B":
                    for half in range(2):
                        rows = slice(half * 64, half * 64 + 64)
                        for acc, src in ((OB, pO), (LB, pL)):
                            if first_branch:
                                kb.copy(acc[rows, :, qsl], src[rows, 2 * half:2 * half + 2, :])
                            else:
                                kb.tt(acc[rows, :, qsl], acc[rows, :, qsl], src[rows, 2 * half:2 * half + 2, :], ALU.add)
                else:
                    for p in range(4):
                        j = HORD[p]
                        rows = slice((j % 2) * 64, (j % 2) * 64 + 64)
                        dd = nxt("dn", dn)
                        kb.ts(dd[rows, :], pL[rows, p, :], esink[rows, j // 2:j // 2 + 1], ALU.add)
                        kb.recip(dd[rows, :], dd[rows, :])
                        kb.tt(Oout[0][rows, j // 2, qsl], pO[rows, p, :], dd[rows, :], ALU.mult)

        for n in range(nspan):
            for ti in range(SPAN // PT if do_proj else 0):
                proj_tile(n, ti)
            first = True
            for d in ((16, 4, 1) if do_attn else ()):
                nblk = SPAN // (128 * d)
                for bl in range(nblk):
                    b = n * nblk + bl
                    for r in range(d):
                        group(n, d, b, r, first)
                first = False
            kb.recip(LB[:], LB[:])
            kb.tt(Oout[1][:], OB[:], LB[:], ALU.mult)
            c0 = n * SPAN
            kb.dma([(oT[0:256, c0:c0 + SPAN].rearrange("(c p) t -> p c t", p=128), Oout[0][:]),
                    ], queue="pool")
            kb.dma([(oT[256:512, c0:c0 + SPAN].rearrange("(c p) t -> p c t", p=128), Oout[1][:]),
                    ], queue="pool")
        kb.finish(["oT"])
        pass
    return nc


def alibi(n=16):
    return (2.0 ** (-8.0 * np.arange(1, n + 1) / n)).astype(np.float32)


def dist_tables():
    k = np.arange(128)[:, None].astype(np.float64)
    q = np.arange(128)[None, :].astype(np.float64)
    out = np.zeros((128, 4, 128), np.float32)
    for i, (off, maxd) in enumerate([(0, 127), (128, 127), (0, 128), (128, 128)]):
        dd = q - k + off
        out[:, i, :] = np.where((dd >= 0) & (dd <= maxd), dd, BIG)
    return out


def blockones():
    m = np.zeros((128, 128), np.float32)
    m[0:64, 0:64] = 1
    m[64:128, 64:128] = 1
    return m


def prep_m0(inp):
    sl = alibi()
    maps = []
    w_in = inp["attn_w_in"][0]
    off = np.cumsum([0, 512, 128, 128, 512, 512, 512])
    for core in range(8):
        s, hh = core // 2, core % 2
        qa = w_in[:, off[0] + 256 * hh: off[0] + 256 * hh + 256]
        ka = w_in[:, off[1] + 64 * hh: off[1] + 64 * hh + 64]
        va = w_in[:, off[2] + 64 * hh: off[2] + 64 * hh + 64]
        qb = w_in[:, off[3] + 256 * hh: off[3] + 256 * hh + 256]
        kbw = w_in[:, off[4] + 256 * hh: off[4] + 256 * hh + 256]
        vb = w_in[:, off[5] + 256 * hh: off[5] + 256 * hh + 256]
        win = np.concatenate([qa, ka, ka, va, va, qb, kbw, vb], axis=1)
        col = lambda v: np.ascontiguousarray(v.reshape(-1, 128).T)
        nrm = np.stack([np.tile(inp["attn_q_norm_a"][0], 2), np.tile(inp["attn_k_norm_a"][0], 2),
                        np.tile(inp["attn_q_norm_b"][0], 2), np.tile(inp["attn_k_norm_b"][0], 2)], axis=1)
        sk = inp["attn_sinks"][0][4 * hh:4 * hh + 4]
        sinks = np.stack([np.repeat(sk[0:2], 64), np.repeat(sk[2:4], 64)], axis=1)
        ns = np.zeros((128, 16), np.float32)
        for j in range(4):
            ns[:, j] = -sl[4 * hh + j]
            for di, d in enumerate((1, 4, 16)):
                ns[:, 4 + di * 4 + j] = -sl[8 + 4 * hh + j] * d
        maps.append({
            "xT": np.ascontiguousarray(inp["x"][s].T),
            "ccol": col(inp["c"][s]),
            "adaw": np.ascontiguousarray(inp["ada_w"][0][:, 0:2048]),
            "adab": col(inp["ada_b"][0][0:2048]),
            "nmix": col(inp["norm_mix"][0]),
            "win": np.ascontiguousarray(win),
            "nrm": np.ascontiguousarray(nrm.astype(np.float32)),
            "sinks": np.ascontiguousarray(sinks.astype(np.float32)),
            "dist": dist_tables(),
            "nslope": ns,
            "ident": np.eye(128, dtype=np.float32),
            "bones": blockones(),
        })
    return maps


def post_m0(results):
    out = np.zeros((4, 1024, SEQ), ml_dtypes.bfloat16)
    for core in range(8):
        s, hh = core // 2, core % 2
        o = results[core]["oT"]
        out[s, 256 * hh:256 * hh + 256] = o[0:256]
        out[s, 512 + 256 * hh:512 + 256 * hh + 256] = o[256:512]
    return out


import ml_dtypes

TOK = 4096
HALO = 2
NCOL = TOK + HALO
EPS = 1e-6
DFF = 2816
NCC = DFF // 128


def build_ff(glu=False):
    nc = bass.Bass("TRN2", target_bir_lowering=False)
    es = ExitStack()
    with es:
        kb = KB(nc, es)
        din = lambda n, s, dt=F32: nc.dram_tensor(n, list(s), dt, kind="ExternalInput").ap()
        xT = din("xT", [1024, NCOL])
        yT = din("yT", [1024, NCOL], BF16)
        flag = din("flag", [128, 1])
        ccol = din("ccol", [128, 8])
        adaw = din("adaw", [1024, 4096])
        adab = din("adab", [128, 32])
        nffn = din("nffn", [128, 8])
        wout = din("wout", [1024, 1024])
        wup = din("wup", [1024, 2 * DFF])
        convw = din("convw", [128, 3, 2 * NCC])
        wdown = din("wdown", [DFF, 1024])
        if glu:
            gluw = din("gluw", [256, 256])
            glub = din("glub", [128, 2])
        xoT = nc.dram_tensor("xoT", [1024, TOK], F32, kind="ExternalOutput").ap()
        xmid = nc.dram_tensor("xmid", [1024, NCOL], F32, kind="Internal").ap()
        gT = nc.dram_tensor("gT", [DFF, TOK], BF16, kind="Internal").ap()

        banks = [kb.ps(f"bk{i}", [128, 512]) for i in range(8)]
        small = kb.sb("small", [128, 64])
        kb.dma([(small[:, 0:8], ccol), (small[:, 8:40], adab), (small[:, 40:48], nffn), (small[:, 48:49], flag)])
        c_sb, adab_sb, nffn_sb, flag_sb = small[:, 0:8], small[:, 8:40], small[:, 40:48], small[:, 48:49]
        cw = kb.sb("cw", [128, 3, 2 * NCC])
        kb.dma([(cw[:], convw)])
        onesb = kb.sb("onesb", [128, 128], BF16)
        kb.memset(onesb[:], 1.0)
        epsc = kb.sb("epsc", [128, 1])
        kb.memset(epsc[:], EPS)
        silc = kb.sb("silc", [128, 8])
        kb.act(silc[:], c_sb, AF.Silu)
        modT = kb.sb("modT", [128, 32])
        cnt = {}

        def nxt(name, ring):
            i = cnt.get(name, 0)
            cnt[name] = i + 1
            return ring[i % len(ring)]

        with ExitStack() as es1:
            kb.es = es1
            xts = [kb.sb(f"xt{i}", [128, 8, 512]) for i in range(2)]
            wr = [xts[i][:, :, 0:128] for i in range(2)]
            emit_mod(kb, nc, adaw, adab_sb, silc, 32, banks[7], modT, wr)
            G1, B2, G2 = modT[:, 0:8], modT[:, 8:16], modT[:, 24:32]
            A2 = kb.sb("A2", [128, 8])
            kb.stt(A2[:], modT[:, 16:24], 1.0, nffn_sb, ALU.add, ALU.mult)
            woutb = kb.sb("woutb", [128, 8, 1024], BF16)
            wupb = kb.sb("wupb", [128, 8, 2 * DFF], BF16)
            for i in range(2):
                st = xts[i % 2]
                kb.dma([(st[:], wout[:, i * 512:(i + 1) * 512].rearrange("(c p) n -> p c n", p=128))])
                kb.copy(woutb[:, :, i * 512:(i + 1) * 512], st[:], eng="pool")
            for i in range(11):
                st = xts[i % 2]
                kb.dma([(st[:], wup[:, i * 512:(i + 1) * 512].rearrange("(c p) n -> p c n", p=128))])
                kb.copy(wupb[:, :, i * 512:(i + 1) * 512], st[:], eng="pool")
            if glu:
                gst = kb.sb("gst", [128, 2, 256])
                gluwb = kb.sb("gluwb", [128, 2, 256], BF16)
                glub_sb = kb.sb("glub_sb", [128, 2])
                kb.dma([(gst[:], gluw.rearrange("(c p) n -> p c n", p=128))])
                kb.dma([(glub_sb[:], glub)])
                kb.copy(gluwb[:], gst[:])
                ycg = [kb.sb(f"ycg{i}", [128, 2, 512], BF16) for i in range(2)]
                sgs = [kb.sb(f"sg{i}", [128, 512]) for i in range(2)]
            yts = [kb.sb(f"yt{i}", [128, 8, 512], BF16) for i in range(2)]
            h2s = [kb.sb("h2", [128, 8, 512], BF16)]
            sqs = [kb.sb(f"sq{i}", [128, 512], BF16) for i in range(3)]
            rts = [kb.sb(f"rt{i}", [128, 512]) for i in range(2)]
            tmps = [kb.sb(f"tmp{i}", [128, 512]) for i in range(3)]
            tas = [kb.sb(f"ta{i}", [128, 512]) for i in range(2)]
            tbs = [kb.sb(f"tb{i}", [128, 512]) for i in range(2)]
            sas = [kb.sb(f"sa{i}", [128, 512]) for i in range(2)]
            gs = [kb.sb(f"g{i}", [128, 512], BF16) for i in range(3)]
            pring = banks[0:2]
            upA = [banks[2], banks[4]]
            upB = [banks[3], banks[5]]
            pss = banks[6]

            tiles = []
            c0 = 0
            while c0 + 2 < NCOL:
                n = min(512, NCOL - c0)
                tiles.append((c0, n))
                c0 += n - 2
            for ti, (c0, n) in enumerate(tiles):
                xt = xts[ti % 2]
                yt = yts[ti % 2]
                kb.dma([(xt[:, :, 0:n], xT[:, c0:c0 + n].rearrange("(c p) t -> p c t", p=128))])
                kb.dma([(yt[:, :, 0:n], yT[:, c0:c0 + n].rearrange("(c p) t -> p c t", p=128))])
                rhs = [yt[:, k, 0:n] for k in range(8)]
                if glu:
                    yc = ycg[ti % 2]
                    for oc in range(2):
                        ps = nxt("pr", pring)
                        for k in range(2):
                            kb.mm(ps[:, 0:n], gluwb[:, k, oc * 128:(oc + 1) * 128], yt[:, k, 0:n], start=(k == 0), stop=(k == 1))
                        sg = nxt("sg", sgs)
                        kb.act(sg[:, 0:n], ps[:, 0:n], AF.Sigmoid, bias=glub_sb[:, oc:oc + 1])
                        kb.tt(yc[:, oc, 0:n], yt[:, oc, 0:n], sg[:, 0:n], ALU.mult)
                        rhs[oc] = yc[:, oc, 0:n]
                for oc in range(8):
                    ps = nxt("pr", pring)
                    for k in range(8):
                        kb.mm(ps[:, 0:n], woutb[:, k, oc * 128:(oc + 1) * 128], rhs[k], start=(k == 0), stop=(k == 7))
                    kb.stt(xt[:, oc, 0:n], ps[:, 0:n], G1[:, oc:oc + 1], xt[:, oc, 0:n], ALU.mult, ALU.add)
                kb.dma([(xmid[:, c0:c0 + n].rearrange("(c p) t -> p c t", p=128), xt[:, :, 0:n])], queue="pool")
                for c in range(8):
                    sq = nxt("sq", sqs)
                    kb.act(sq[:, 0:n], xt[:, c, 0:n], AF.Square)
                    kb.mm(pss[:, 0:n], onesb[:], sq[:, 0:n], start=(c == 0), stop=(c == 7))
                r = nxt("rt", rts)
                kb.act(r[:, 0:n], pss[:, 0:n], AF.Sqrt, bias=epsc[:, 0:1], scale=1.0 / 1024.0)
                kb.recip(r[:, 0:n], r[:, 0:n])
                h2 = h2s[0]
                for c in range(8):
                    tm = nxt("tmp", tmps)
                    kb.tt(tm[:, 0:n], xt[:, c, 0:n], r[:, 0:n], ALU.mult)
                    kb.act(h2[:, c, 0:n], tm[:, 0:n], AF.Identity, bias=B2[:, c:c + 1], scale=A2[:, c:c + 1])
                if ti == 0:
                    kb.ts(h2[:, :, 0:2], h2[:, :, 0:2], flag_sb, ALU.mult)
                m = n - 2
                for cc in range(NCC):
                    pa = nxt("upA", upA)
                    pb = nxt("upB", upB)
                    for k in range(8):
                        kb.mm(pa[:, 0:n], wupb[:, k, cc * 128:(cc + 1) * 128], h2[:, k, 0:n], start=(k == 0), stop=(k == 7))
                    for k in range(8):
                        kb.mm(pb[:, 0:n], wupb[:, k, (NCC + cc) * 128:(NCC + cc + 1) * 128], h2[:, k, 0:n], start=(k == 0), stop=(k == 7))
                    ta = nxt("ta", tas)
                    tb = nxt("tb", tbs)
                    for (pp, tt_, ch) in ((pa, ta, cc), (pb, tb, NCC + cc)):
                        kb.act(tt_[:, 0:m], pp[:, 2:n], AF.Identity, scale=cw[:, 2, ch:ch + 1])
                        kb.stt(tt_[:, 0:m], pp[:, 1:n - 1], cw[:, 1, ch:ch + 1], tt_[:, 0:m], ALU.mult, ALU.add)
                        kb.stt(tt_[:, 0:m], pp[:, 0:n - 2], cw[:, 0, ch:ch + 1], tt_[:, 0:m], ALU.mult, ALU.add)
                    sa = nxt("sa", sas)
                    kb.act(sa[:, 0:m], ta[:, 0:m], AF.Silu)
                    g = nxt("g", gs)
                    kb.tt(g[:, 0:m], sa[:, 0:m], tb[:, 0:m], ALU.mult, eng="pool")
                    kb.dma([(gT[cc * 128:(cc + 1) * 128, c0:c0 + m], g[:, 0:m])], queue="pool")
        kb.barrier()
        with ExitStack() as es2:
            kb.es = es2
            wdb = kb.sb("wdb", [128, NCC, 1024], BF16)
            xms = [kb.sb(f"xm{i}", [128, 8, 512]) for i in range(2)]
            gts = [kb.sb(f"gt{i}", [128, NCC, 512], BF16) for i in range(2)]
            for i in range(6):
                kc = min(4, NCC - 4 * i)
                st = xms[i % 2].rearrange("p c t -> p (c t)")[:, 0:kc * 1024].rearrange("p (c n) -> p c n", c=kc)
                kb.dma([(st, wdown[i * 512:i * 512 + kc * 128, :].rearrange("(c p) n -> p c n", p=128))])
                kb.copy(wdb[:, 4 * i:4 * i + kc, :], st, eng="pool")
            pring = banks[0:3]
            for ti in range(TOK // 512):
                t0 = ti * 512
                xm = xms[ti % 2]
                gt = gts[ti % 2]
                kb.dma([(xm[:], xmid[:, HALO + t0:HALO + t0 + 512].rearrange("(c p) t -> p c t", p=128))])
                kb.dma([(gt[:], gT[:, t0:t0 + 512].rearrange("(c p) t -> p c t", p=128))])
                for oc in range(8):
                    ps = nxt("p2", pring)
                    for k in range(NCC):
                        kb.mm(ps[:], wdb[:, k, oc * 128:(oc + 1) * 128], gt[:, k, :], start=(k == 0), stop=(k == NCC - 1))
                    kb.stt(xm[:, oc, :], ps[:], G2[:, oc:oc + 1], xm[:, oc, :], ALU.mult, ALU.add)
                kb.dma([(xoT[:, t0:t0 + 512].rearrange("(c p) t -> p c t", p=128), xm[:])], queue="pool")
        kb.es = es
        kb.finish(["xoT"])
        pass
    return nc


def col(v):
    return np.ascontiguousarray(np.asarray(v, np.float32).reshape(-1, 128).T)


def prep_ff(inp, layer, xs, yT, glu=False):
    maps = []
    wout = inp["attn_w_out"][0] if layer == 0 else inp["rec_w_out"][0]
    cv = inp["ffn_conv"][layer]
    convw = np.ascontiguousarray(cv.reshape(3, 2 * NCC, 128).transpose(2, 0, 1))
    for core in range(8):
        s, half = core // 2, core % 2
        lo = half * TOK - HALO
        xt = np.zeros((1024, NCOL), np.float32)
        yt = np.zeros((1024, NCOL), ml_dtypes.bfloat16)
        if half == 0:
            xt[:, HALO:] = xs[s, 0:TOK].T
            yt[:, HALO:] = yT[s][:, 0:TOK]
        else:
            xt[:] = xs[s, lo:lo + NCOL].T
            yt[:] = yT[s][:, lo:lo + NCOL]
        m = {
            "xT": xt, "yT": yt,
            "flag": np.full((128, 1), float(half), np.float32),
            "ccol": col(inp["c"][s]),
            "adaw": np.ascontiguousarray(inp["ada_w"][layer][:, 2048:6144]),
            "adab": col(inp["ada_b"][layer][2048:6144]),
            "nffn": col(inp["norm_ffn"][layer]),
            "wout": np.ascontiguousarray(wout),
            "wup": np.ascontiguousarray(inp["ffn_w_up"][layer]),
            "convw": convw,
            "wdown": np.ascontiguousarray(inp["ffn_w_down"][layer]),
        }
        if glu:
            m["gluw"] = np.ascontiguousarray(inp["s5_glu_w"][0])
            m["glub"] = col(inp["s5_glu_b"][0])
        maps.append(m)
    return maps


def post_ff(results):
    out = np.zeros((4, 2 * TOK, 1024), np.float32)
    for core in range(8):
        s, half = core // 2, core % 2
        out[s, half * TOK:(half + 1) * TOK] = results[core]["xoT"].T
    return out


import ml_dtypes, math, os

SEQ = 8192
NT = 256
NCH = NT // 64
EPS = 1e-6
MAGIC = 12582912.0
TWO_PI = 2.0 * math.pi


def build_m1(L=SEQ, do_s5=True, do_dn=True, dbg=int(os.environ.get('M1_DBG', '99'))):
    nc = bass.Bass("TRN2", target_bir_lowering=False)
    es = ExitStack()
    with es:
        kb = KB(nc, es)
        din = lambda n, s, dt=F32: nc.dram_tensor(n, list(s), dt, kind="ExternalInput").ap()
        xT = din("xT", [1024, L])
        ccol = din("ccol", [128, 8])
        adaw = din("adaw", [1024, 2048])
        adab = din("adab", [128, 16])
        nmix = din("nmix", [128, 8])
        win = din("win", [1024, 1664])
        wab = din("wab", [1024, 6])
        convw = din("convw", [128, 4, 9])
        dtb = din("dtb", [128, 3])
        alog = din("alog", [128, 3])
        onorm = din("onorm", [128, 128])
        s5rl = din("s5rl", [128, 3, 512])
        s5cl = din("s5cl", [128, 3, 8])
        s5b = din("s5b", [128, 2, 512])
        s5c = din("s5c", [128, 2, 8, 128])
        dskip = din("dskip", [128, 1])
        consts = din("consts", [128, 5, 128])
        iota = din("iota", [128, NT])
        gaT = nc.dram_tensor("gaT", [128, L], BF16, kind="ExternalOutput").ap()
        ydT = nc.dram_tensor("ydT", [384, L], BF16, kind="ExternalOutput").ap()

        banks = [kb.ps(f"bk{i}", [128, 512]) for i in range(8)]
        ring = banks[0:4]
        bkY = banks[4]
        rec = banks[5:8]
        cnt = {}

        def nxt(name, rg):
            i = cnt.get(name, 0)
            cnt[name] = i + 1
            return rg[i % len(rg)]

        def pbank():
            return nxt("ring", ring)

        small = kb.sb("small", [128, 64])
        kb.dma([(small[:, 0:8], ccol), (small[:, 8:24], adab), (small[:, 24:32], nmix),
                (small[:, 32:35], dtb), (small[:, 35:38], alog), (small[:, 38:39], dskip)])
        c_sb, adab_sb, nmix_sb = small[:, 0:8], small[:, 8:24], small[:, 24:32]
        dtb_sb, alog_sb, dsk_sb = small[:, 32:35], small[:, 35:38], small[:, 38:39]
        cf = kb.sb("cf", [128, 5, 128])
        kb.dma([(cf[:], consts)])
        identf, pswapf, triuf, negmask, strict = (cf[:, i, :] for i in range(5))
        cb = kb.sb("cb", [128, 128], BF16)
        kb.copy(cb[:], cf[:, 0, :])
        identb = cb[:]
        onesb = kb.sb("onesb", [128, 128], BF16)
        kb.memset(onesb[:], 1.0)
        onesf = kb.sb("onesf", [128, 256])
        kb.memset(onesf[:], 1.0)
        epsc = kb.sb("epsc", [128, 1])
        kb.memset(epsc[:], EPS)
        onec = kb.sb("onec", [128, 1])
        kb.memset(onec[:], 1.0)
        silc = kb.sb("silc", [128, 8])
        kb.act(silc[:], c_sb, AF.Silu)
        xts = [kb.sb(f"xt{i}", [128, 8, NT]) for i in range(2)]
        wr = [xts[i][:, :, 0:128] for i in range(2)]
        modT = kb.sb("modT", [128, 16])
        emit_mod(kb, nc, adaw, adab_sb, silc, 16, banks[7], modT, wr)
        A1 = kb.sb("A1", [128, 8])
        kb.stt(A1[:], modT[:, 8:16], 1.0, nmix_sb, ALU.add, ALU.mult)
        B1 = modT[:, 0:8]
        wb = kb.sb("wb", [128, 8, 1664], BF16)
        for oc in range(13):
            w = wr[oc % 2]
            kb.dma([(w, win[:, oc * 128:(oc + 1) * 128].rearrange("(c p) n -> p c n", p=128))])
            kb.copy(wb[:, :, oc * 128:(oc + 1) * 128], w, eng="pool")
        wabf = kb.sb("wabf", [128, 8, 6])
        kb.dma([(wabf[:], wab.rearrange("(c p) n -> p c n", p=128))])
        wabb = kb.sb("wabb", [128, 8, 6], BF16)
        kb.copy(wabb[:], wabf[:])
        cw = kb.sb("cw", [128, 4, 9])
        kb.dma([(cw[:], convw)])
        onr = kb.sb("onr", [128, 128])
        kb.dma([(onr[:], onorm)])

        def rr(dst, src, tmp):
            kb.ts(tmp, src, 1.0 / TWO_PI, ALU.mult, MAGIC, ALU.add)
            kb.ts(tmp, tmp, MAGIC, ALU.subtract)
            kb.stt(dst, tmp, -TWO_PI, src, ALU.mult, ALU.add)

        if do_s5:
            L1b = kb.sb("L1b", [128, 8, 128], BF16)
            L2b = kb.sb("L2b", [128, 8, 128], BF16)
            C1b = kb.sb("C1b", [128, 8, 128], BF16)
            C2b = kb.sb("C2b", [128, 8, 128], BF16)
            CosT = kb.sb("CosT", [128, 8, NT])
            SinT = kb.sb("SinT", [128, 8, NT])
            cl = kb.sb("cl", [128, 6, 8])
            inits = [kb.sb(f"s5init{i}", [128, 8]) for i in range(2)]
            kb.memset(inits[0][:], 0.0)
            zl = kb.sb("zl", [128, 8])
            with ExitStack() as es_s:
                kb.es = es_s
                rl = kb.sb("rl", [128, 3, 512])
                kb.dma([(rl[:], s5rl)])
                bt = kb.sb("bt", [128, 2, 512])
                kb.dma([(bt[:], s5b)])
                ct = kb.sb("ct", [128, 2, 8, 128])
                kb.dma([(ct[:], s5c)])
                clin = kb.sb("clin", [128, 3, 8])
                kb.dma([(clin[:], s5cl)])
                io = kb.sb("io", [128, NT])
                kb.dma([(io[:], iota)])
                T = [kb.sb(f"s5t{i}", [128, 512]) for i in range(10)]
                lr, li, ldt = rl[:, 0, :], rl[:, 1, :], rl[:, 2, :]
                dt_, rho, th, cs, sn, nr, ni, den, fre, fim = (t[:] for t in T)
                kb.act(dt_, ldt, AF.Exp)
                kb.tt(th, lr, dt_, ALU.mult)
                kb.act(rho, th, AF.Exp)
                kb.tt(th, li, dt_, ALU.mult)
                kb.ts(nr, th, math.pi / 2, ALU.add)
                rr(den, nr, ni)
                kb.act(cs, den, AF.Sin)
                rr(den, th, ni)
                kb.act(sn, den, AF.Sin)
                kb.tt(nr, rho, cs, ALU.mult)
                kb.ts(nr, nr, -1.0, ALU.add)
                kb.tt(ni, rho, sn, ALU.mult)
                kb.tt(den, lr, lr, ALU.mult)
                kb.tt(fre, li, li, ALU.mult)
                kb.tt(den, den, fre, ALU.add)
                kb.recip(den, den)
                kb.tt(fre, nr, lr, ALU.mult)
                kb.tt(fim, ni, li, ALU.mult)
                kb.tt(fre, fre, fim, ALU.add)
                kb.tt(fre, fre, den, ALU.mult)
                kb.tt(fim, ni, lr, ALU.mult)
                kb.tt(cs, nr, li, ALU.mult)
                kb.tt(fim, fim, cs, ALU.subtract)
                kb.tt(fim, fim, den, ALU.mult)
                L1f = kb.sb("L1f", [128, 8, 128])
                bre, bim = bt[:, 0, :], bt[:, 1, :]
                v3 = lambda a: a.rearrange("p (g q) -> p g q", g=8)
                kb.tt(cs, fre, bre, ALU.mult)
                kb.tt(sn, fim, bim, ALU.mult)
                kb.tt(L1f[:, :, 0:64], v3(cs), v3(sn), ALU.subtract)
                kb.tt(cs, fre, bim, ALU.mult)
                kb.tt(sn, fim, bre, ALU.mult)
                kb.tt(L1f[:, :, 64:128], v3(cs), v3(sn), ALU.add)
                kb.copy(L1b[:], L1f[:])
                kb.copy(L2b[:, :, 0:64], L1f[:, :, 64:128])
                kb.ts(L2b[:, :, 64:128], L1f[:, :, 0:64], -1.0, ALU.mult)
                kb.copy(C1b[0:64], ct[0:64, 0])
                kb.ts(C1b[64:128], ct[64:128, 0], -1.0, ALU.mult)
                kb.ts(C2b[:], ct[:, 1], -1.0, ALU.mult)
                kb.act(cl[:, 4, :], clin[:, 2, :], AF.Exp)
                kb.tt(cl[:, 5, :], clin[:, 0, :], cl[:, 4, :], ALU.mult)
                kb.act(cl[:, 0, :], cl[:, 5, :], AF.Exp)
                kb.tt(cl[:, 1, :], clin[:, 1, :], cl[:, 4, :], ALU.mult)
                ang, a2, tmp = T[0][:, 0:NT], T[1][:, 0:NT], T[2][:, 0:NT]
                for g in range(8):
                    kb.ts(ang, io[:], cl[:, 1, g:g + 1], ALU.mult)
                    rr(a2, ang, tmp)
                    kb.act(SinT[:, g, :], a2, AF.Sin)
                    kb.ts(ang, ang, math.pi / 2, ALU.add)
                    rr(a2, ang, tmp)
                    kb.act(CosT[:, g, :], a2, AF.Sin)
                a8, b8, t8 = T[3][:, 0:8], T[4][:, 0:8], T[5][:, 0:8]
                kb.ts(a8, cl[:, 1, :], float(NT), ALU.mult)
                rr(b8, a8, t8)
                kb.act(cl[:, 3, :], b8, AF.Sin)
                kb.ts(a8, a8, math.pi / 2, ALU.add)
                rr(b8, a8, t8)
                kb.act(cl[:, 2, :], b8, AF.Sin)
            kb.es = es
            kb.barrier()
            rhoB = kb.sb("rhoB", [128, 8, NT])
            for g in range(8):
                kb.ts(rhoB[:, g, :], onesf[:, 0:NT], cl[:, 0, g:g + 1], ALU.mult)
            ufs = [kb.sb(f"uf{i}", [128, NT]) for i in range(2)]
            ubs = [kb.sb(f"ub{i}", [128, NT], BF16) for i in range(2)]
            t1s = [kb.sb(f"t1_{i}", [128, NT]) for i in range(2)]
            t2s = [kb.sb(f"t2_{i}", [128, NT]) for i in range(2)]
            zs_ = [kb.sb(f"z_{i}", [128, NT]) for i in range(2)]
            w1s = [kb.sb(f"w1_{i}", [128, NT], BF16) for i in range(2)]
            w2s = [kb.sb(f"w2_{i}", [128, NT], BF16) for i in range(2)]
            gas = [kb.sb(f"ga{i}", [128, NT], BF16) for i in range(2)]
            yfs = [kb.sb(f"yf{i}", [128, NT]) for i in range(2)]

        if do_dn:
            nea = kb.sb("nea", [128, 3])
            kb.act(nea[:], alog_sb, AF.Exp)
            kb.ts(nea[:], nea[:], -1.0, ALU.mult)
            cbufs = [kb.sb(f"cbuf{i}", [128, NT + 3]) for i in range(9)]
            for i in range(9):
                kb.memset(cbufs[i][:, 0:3], 0.0)
            tcs = [kb.sb(f"tc{i}", [128, NT]) for i in range(3)]
            qkv = kb.sb("qkv", [128, 9, NT], BF16)
            zsil = kb.sb("zsil", [64, NCH, 384])
            S = [kb.sb(f"S{h}", [128, 128]) for h in range(3)]
            Sb = [kb.sb(f"Sb{h}", [128, 128], BF16) for h in range(3)]
            for h in range(3):
                kb.memset(S[h][:], 0.0)
                kb.memset(Sb[h][:], 0.0)
            ydt = [kb.sb(f"ydt{i}", [128, 3, NT], BF16) for i in range(2)]
            gt = kb.sb("gtok", [64, NCH, 3])
            bt_ = kb.sb("btok", [64, NCH, 3])
            xab = kb.sb("xab", [64, NCH, 3])
            def hs(name, shape, dt=F32):
                return [kb.sb(f"{name}{h}", shape, dt) for h in range(3)]
            Gc, eG, beG, ekd, nbeta = hs("Gc", [64, NCH]), hs("eG", [64, NCH]), hs("beG", [64, NCH]), hs("ekd", [64, NCH]), hs("nbeta", [64, NCH])
            egl = hs("egl", [128, NCH])
            kbe, kdc, vbt = hs("kbe", [64, NCH, 128], BF16), hs("kdc", [64, NCH, 128], BF16), hs("vbt", [64, NCH, 128], BF16)
            ATb, Yb = hs("ATb", [64, NCH, 64], BF16), hs("Yb", [64, NCH, 64], BF16)
            wTb = hs("wTb", [128, NCH, 64], BF16)
            uS = hs("uS", [64, NCH, 128])
            oS = hs("oS", [64, NCH, 128])
            vn = hs("vn", [64, 128], BF16)
            o1 = hs("o1", [64, 128])
            gB, ngB = kb.sb("gB", [64, NCH, 64]), kb.sb("ngB", [64, NCH, 64])
            dm, Gm, GT, GS = (kb.sb(n, [64, NCH, 64]) for n in ("dm", "Gm", "GT", "GS"))
            Pm = [kb.sb(f"Pm{i}", [64, NCH, 64]) for i in range(2)]
            Zm = [kb.sb(f"Zm{i}", [64, NCH, 64]) for i in range(2)]
            Ym = [kb.sb(f"Ym{i}", [64, NCH, 64]) for i in range(2)]
            ssq = kb.sb("ssq", [64, NCH])
            junk = kb.sb("junk", [64, 128])
            yn = kb.sb("yn", [64, NCH, 128])
            ynb = kb.sb("ynb", [64, NCH, 128], BF16)

        hTs = [kb.sb(f"hT{i}", [128, 8, NT], BF16) for i in range(2)]
        sqs = [kb.sb(f"sq{i}", [128, NT], BF16) for i in range(3)]
        rts = [kb.sb(f"rt{i}", [128, NT]) for i in range(3)]

        def bview(ps_bank):
            return ps_bank.bitcast(BF16)

        ntiles = L // NT if dbg > 0 else 0
        for ti in range(ntiles):
            t0 = ti * NT
            xt = xts[ti % 2]
            kb.dma([(xt[:], xT[:, t0:t0 + NT].rearrange("(c p) t -> p c t", p=128))])
            pss = pbank()
            for c in range(8):
                sq = nxt("sq", sqs)
                kb.act(sq[:], xt[:, c, :], AF.Square)
                kb.mm(pss[:, 0:NT], onesb[:], sq[:], start=(c == 0), stop=(c == 7))
            r = nxt("rt", rts)
            kb.act(r[:], pss[:, 0:NT], AF.Sqrt, bias=epsc[:, 0:1], scale=1.0 / 1024.0)
            kb.recip(r[:], r[:])
            kb.tt(xt[:], xt[:], r[:].unsqueeze(1).to_broadcast([128, 8, NT]), ALU.mult)
            hT = hTs[ti % 2]
            for c in range(8):
                kb.act(hT[:, c, :], xt[:, c, :], AF.Identity, bias=B1[:, c:c + 1], scale=A1[:, c:c + 1])

            if do_s5:
                psU = pbank()
                for k in range(8):
                    kb.mm(psU[:, 0:NT], wb[:, k, 0:128], hT[:, k, :], start=(k == 0), stop=(k == 7))
                uf, ub = ufs[ti % 2], ubs[ti % 2]
                kb.copy(uf[:], psU[:, 0:NT], eng="act")
                kb.copy(ub[:], psU[:, 0:NT])
                init = inits[ti % 2]
                for g in range(8 if dbg >= 2 else 0):
                    psE = pbank()
                    kb.mm(psE[:, 0:NT], L1b[:, g, :], ub[:])
                    kb.mm(psE[:, NT:2 * NT], L2b[:, g, :], ub[:])
                    t1, t2, z = nxt("t1", t1s), nxt("t2", t2s), nxt("z", zs_)
                    kb.tt(t1[:], psE[:, 0:NT], CosT[:, g, :], ALU.mult)
                    kb.tt(t2[:], psE[:, NT:2 * NT], SinT[:, g, :], ALU.mult)
                    kb.tt(t1[:], t1[:], t2[:], ALU.add, eng="pool")
                    if dbg < 3:
                        continue
                    kb.scan(z[:], rhoB[:, g, :], t1[:], init[:, g:g + 1])
                    if dbg < 4:
                        continue
                    w1, w2 = nxt("w1", w1s), nxt("w2", w2s)
                    kb.tt(w1[:], z[:], CosT[:, g, :], ALU.mult, eng="pool")
                    kb.tt(w2[:], z[:], SinT[:, g, :], ALU.mult, eng="pool")
                    kb.mm(bkY[:, 0:NT], C1b[:, g, :], w1[:], start=(g == 0), stop=False)
                    kb.mm(bkY[:, 0:NT], C2b[:, g, :], w2[:], start=False, stop=(g == 7))
                    kb.copy(zl[:, g:g + 1], z[:, NT - 1:NT], eng="act")
                if dbg < 5:
                    continue
                psW = pbank()
                kb.mm(psW[:, 0:8], pswapf, zl[:])
                ninit = inits[(ti + 1) % 2]
                kb.tt(ninit[:], zl[:], cl[:, 2, :], ALU.mult)
                kb.tt(cl[:, 4, :], psW[:, 0:8], cl[:, 3, :], ALU.mult)
                kb.tt(ninit[:], ninit[:], cl[:, 4, :], ALU.add)
                if dbg < 6:
                    continue
                yf = yfs[ti % 2]
                kb.stt(yf[:], uf[:], dsk_sb, bkY[:, 0:NT], ALU.mult, ALU.add)
                ga = gas[ti % 2]
                g1_, g2_ = nxt("t1", t1s), nxt("t2", t2s)
                kb.tt(g1_[:], yf[:], yf[:], ALU.mult)
                kb.ts(g1_[:], g1_[:], 0.044715, ALU.mult, 1.0, ALU.add)
                kb.tt(g1_[:], g1_[:], yf[:], ALU.mult)
                kb.act(g2_[:], g1_[:], AF.Sigmoid, scale=2.0 * math.sqrt(2.0 / math.pi))
                kb.tt(ga[:], yf[:], g2_[:], ALU.mult)
                kb.dma([(gaT[:, t0:t0 + NT], ga[:])], queue="pool")

            if not do_dn:
                continue
            for ci in range(9):
                ps = pbank()
                for k in range(8):
                    kb.mm(ps[:, 0:NT], wb[:, k, 128 + ci * 128:256 + ci * 128], hT[:, k, :], start=(k == 0), stop=(k == 7))
                cbf = cbufs[ci]
                kb.copy(cbf[:, 3:3 + NT], ps[:, 0:NT], eng="act")
                tc_ = nxt("tc", tcs)
                kb.ts(tc_[:], cbf[:, 3:3 + NT], cw[:, 3, ci:ci + 1], ALU.mult)
                for j in (2, 1, 0):
                    kb.stt(tc_[:], cbf[:, j:j + NT], cw[:, j, ci:ci + 1], tc_[:], ALU.mult, ALU.add)
                kb.copy(cbf[:, 0:3], cbf[:, NT:NT + 3])
                if ci >= 6:
                    kb.act(qkv[:, ci, :], tc_[:], AF.Silu)
                    continue
                kb.act(tc_[:], tc_[:], AF.Silu)
                sq = nxt("sq", sqs)
                kb.act(sq[:], tc_[:], AF.Square)
                ps2 = pbank()
                kb.mm(ps2[:, 0:NT], onesb[:], sq[:])
                r = nxt("rt", rts)
                kb.act(r[:], ps2[:, 0:NT], AF.Sqrt, bias=epsc[:, 0:1], scale=1.0)
                kb.recip(r[:], r[:])
                kb.stt(qkv[:, ci, :], tc_[:], (128.0 ** -0.5) if ci < 3 else 1.0, r[:], ALU.mult, ALU.mult)
            psab = bkY[:, NT:2 * NT]
            for c in range(NCH):
                psz = pbank()
                for k in range(8):
                    kb.mm(psz[0:64, 0:384], hT[:, k, c * 64:(c + 1) * 64], wb[:, k, 1280:1664], start=(k == 0), stop=(k == 7))
                kb.act(zsil[:, c, :], psz[0:64, 0:384], AF.Silu)
                for k in range(8):
                    kb.mm(psab[0:64, c * 8:c * 8 + 6], hT[:, k, c * 64:(c + 1) * 64], wabb[:, k, :], start=(k == 0), stop=(k == 7))
            pab = psab[0:64, 0:NCH * 8].rearrange("p (c s) -> p c s", s=8)
            kb.tt(xab[:], pab[:, :, 0:3], dtb_sb[0:64].unsqueeze(1).to_broadcast([64, NCH, 3]), ALU.add)
            kb.act(xab[:], xab[:], AF.Exp)
            kb.act(xab[:], xab[:], AF.Ln, bias=onec[0:64, 0:1])
            kb.tt(gt[:], xab[:], nea[0:64].unsqueeze(1).to_broadcast([64, NCH, 3]), ALU.mult)
            kb.act(bt_[:], pab[:, :, 3:6], AF.Sigmoid)

            for h in range(3):
                gh, bh = gt[:, :, h], bt_[:, :, h]
                qT, kT, vT = qkv[:, h, :], qkv[:, 3 + h, :], qkv[:, 6 + h, :]
                psG = pbank()
                kb.mm(psG[0:64, 0:NCH], triuf[0:64, 0:64], gh)
                kb.mm(psG[:, 8:8 + NCH], onesf[0:64, 0:128], gh)
                kb.copy(Gc[h][:], psG[0:64, 0:NCH])
                kb.act(egl[h][:], psG[:, 8:8 + NCH], AF.Exp)
                kb.act(eG[h][:], psG[0:64, 0:NCH], AF.Exp)
                kb.tt(beG[h][:], eG[h][:], bh, ALU.mult)
                kb.tt(ekd[h][:], psG[0:64, 8:8 + NCH], Gc[h][:], ALU.subtract)
                kb.act(ekd[h][:], ekd[h][:], AF.Exp)
                kb.ts(nbeta[h][:], bh, -1.0, ALU.mult)
                kb.tt(gB[:], onesf[0:64, 0:NCH * 64].rearrange("p (c j) -> p c j", c=NCH),
                      gh.unsqueeze(2).to_broadcast([64, NCH, 64]), ALU.mult)
                kb.ts(ngB[:], gB[:], -1.0, ALU.mult)
                psD = pbank()
                for c in range(NCH):
                    kb.mm(psD[0:64, c * 64:(c + 1) * 64], triuf[0:64, 0:64], gB[:, c, :], start=True, stop=False)
                    kb.mm(psD[0:64, c * 64:(c + 1) * 64], ngB[:, c, :], triuf[0:64, 0:64], start=False, stop=True)
                pD3 = psD[0:64, 0:NCH * 64].rearrange("p (c j) -> p c j", c=NCH)
                kb.stt(dm[:], pD3, 0.0, negmask[0:64, 0:64].unsqueeze(1).to_broadcast([64, NCH, 64]), ALU.min, ALU.add)
                kb.act(Gm[:], dm[:], AF.Exp)
                psT = pbank()
                for c in range(NCH):
                    kb.tr(psT[0:64, c * 64:(c + 1) * 64], Gm[:, c, :], identf[0:64, 0:64])
                kb.copy(GT[:], psT[0:64, 0:NCH * 64].rearrange("p (c j) -> p c j", c=NCH), eng="act")
                kb.tt(GS[:], Gm[:], strict[0:64, 0:64].unsqueeze(1).to_broadcast([64, NCH, 64]), ALU.mult, eng="pool")
                kb.tt(GS[:], GS[:], nbeta[h][:].unsqueeze(2).to_broadcast([64, NCH, 64]), ALU.mult, eng="pool")
                psK = pbank()
                for c in range(NCH):
                    cs_ = slice(c * 64, (c + 1) * 64)
                    kb.mm(psK[0:64, cs_], kT[:, cs_], kT[:, cs_])
                    kb.mm(psK[0:64, NT + c * 64:NT + (c + 1) * 64], kT[:, cs_], qT[:, cs_])
                P, Z, Y = Pm[0], Zm[0], Ym[0]
                kb.tt(P[:], psK[0:64, 0:NT].rearrange("p (c j) -> p c j", c=NCH), GS[:], ALU.mult)
                kb.tt(ATb[h][:], psK[0:64, NT:2 * NT].rearrange("p (c j) -> p c j", c=NCH), GT[:], ALU.mult)
                psZ = pbank()
                for c in range(NCH):
                    kb.tr(psZ[0:64, c * 64:(c + 1) * 64], P[:, c, :], identf[0:64, 0:64])
                pZ3 = psZ[0:64, 0:NT].rearrange("p (c j) -> p c j", c=NCH)
                kb.copy(Z[:], pZ3, eng="act")
                kb.tt(Y[:], pZ3, identf[0:64, 0:64].unsqueeze(1).to_broadcast([64, NCH, 64]), ALU.add)
                for lv in range(1, 6):
                    Pn, Zn, Yn = Pm[lv % 2], Zm[lv % 2], Ym[lv % 2]
                    psP = pbank()
                    for c in range(NCH):
                        kb.mm(psP[0:64, c * 64:(c + 1) * 64], Z[:, c, :], P[:, c, :])
                        if lv < 5:
                            kb.mm(psP[0:64, NT + c * 64:NT + (c + 1) * 64], P[:, c, :], Z[:, c, :])
                    kb.copy(Pn[:], psP[0:64, 0:NT].rearrange("p (c j) -> p c j", c=NCH), eng="act")
                    if lv < 5:
                        kb.copy(Zn[:], psP[0:64, NT:2 * NT].rearrange("p (c j) -> p c j", c=NCH))
                    psYs = pbank()
                    for c in range(NCH):
                        kb.mm(psYs[0:64, c * 64:(c + 1) * 64], Pn[:, c, :], Y[:, c, :])
                    dst = Yb[h] if lv == 5 else Yn
                    kb.tt(dst[:], Y[:], psYs[0:64, 0:NT].rearrange("p (c j) -> p c j", c=NCH), ALU.add)
                    P, Z, Y = Pn, Zn, Yn
                psTk = pbank()
                pTk = bview(psTk)
                for c in range(NCH):
                    cs_ = slice(c * 64, (c + 1) * 64)
                    kb.tr(pTk[0:64, c * 128:(c + 1) * 128], kT[:, cs_], identb)
                    kb.tr(pTk[0:64, 512 + c * 128:512 + (c + 1) * 128], vT[:, cs_], identb)
                pk3 = pTk[0:64, 0:512].rearrange("p (c d) -> p c d", c=NCH)
                pv3 = pTk[0:64, 512:1024].rearrange("p (c d) -> p c d", c=NCH)
                kb.tt(kbe[h][:], pk3, beG[h][:].unsqueeze(2).to_broadcast([64, NCH, 128]), ALU.mult)
                kb.tt(kdc[h][:], pk3, ekd[h][:].unsqueeze(2).to_broadcast([64, NCH, 128]), ALU.mult)
                kb.tt(vbt[h][:], pv3, bh.unsqueeze(2).to_broadcast([64, NCH, 128]), ALU.mult)
                psu = pbank()
                for c in range(NCH):
                    kb.mm(psu[0:64, c * 128:(c + 1) * 128], Yb[h][:, c, :], vbt[h][:, c, :])
                kb.copy(uS[h][:], psu[0:64, 0:512].rearrange("p (c d) -> p c d", c=NCH), eng="act")
                psw = pbank()
                for c in range(NCH):
                    kb.mm(psw[:, c * 64:(c + 1) * 64], kbe[h][:, c, :], Yb[h][:, c, :])
                kb.copy(wTb[h][:], psw[:, 0:NT].rearrange("p (c i) -> p c i", c=NCH))

            for c in range(NCH):
                cs_ = slice(c * 64, (c + 1) * 64)
                for h in range(3):
                    pr = rec[h]
                    kb.mm(pr[0:64, 0:128], wTb[h][:, c, :], Sb[h][:])
                    kb.mm(pr[0:64, 128:256], qkv[:, h, cs_], Sb[h][:])
                for h in range(3):
                    pr = rec[h]
                    kb.tt(vn[h][:], uS[h][:, c, :], pr[0:64, 0:128], ALU.subtract)
                    kb.act(o1[h][:], pr[0:64, 128:256], AF.Identity, scale=eG[h][:, c:c + 1])
                for h in range(3):
                    pr = rec[h]
                    kb.mm(pr[0:64, 256:384], ATb[h][:, c, :], vn[h][:])
                    kb.mm(pr[:, 384:512], kdc[h][:, c, :], vn[h][:])
                for h in range(3):
                    pr = rec[h]
                    kb.tt(oS[h][:, c, :], pr[0:64, 256:384], o1[h][:], ALU.add)
                    kb.stt(S[h][:], S[h][:], egl[h][:, c:c + 1], pr[:, 384:512], ALU.mult, ALU.add)
                    kb.copy(Sb[h][:], S[h][:], eng="act")
            yd = ydt[ti % 2]
            for h in range(3):
                for c in range(NCH):
                    kb.act(junk[:], oS[h][:, c, :], AF.Square, accum_out=ssq[:, c:c + 1])
                kb.act(ssq[:], ssq[:], AF.Sqrt, bias=epsc[0:64, 0:1], scale=1.0 / 128.0)
                kb.recip(ssq[:], ssq[:])
                kb.tt(yn[:], oS[h][:], ssq[:].unsqueeze(2).to_broadcast([64, NCH, 128]), ALU.mult)
                kb.tt(yn[:], yn[:], onr[0:64, :].unsqueeze(1).to_broadcast([64, NCH, 128]), ALU.mult, eng="pool")
                kb.tt(ynb[:], yn[:], zsil[:, :, h * 128:(h + 1) * 128], ALU.mult, eng="pool")
                psy = pbank()
                py = bview(psy)
                for c in range(NCH):
                    kb.tr(py[:, c * 64:(c + 1) * 64], ynb[:, c, :], identb[0:64, 0:64])
                kb.copy(yd[:, h, :], py[:, 0:NT], eng="act")
            kb.dma([(ydT[:, t0:t0 + NT].rearrange("(h p) t -> p h t", p=128), yd[:])], queue="pool")
        kb.finish(["gaT", "ydT"])
        pass
    return nc


def col(v):
    return np.ascontiguousarray(np.asarray(v, np.float32).reshape(-1, 128).T)


def m1_consts():
    c = np.zeros((128, 5, 128), np.float32)
    c[:, 0, :] = np.eye(128)
    for p in range(64):
        c[64 + p, 1, p] = -1.0
        c[p, 1, 64 + p] = 1.0
    t = np.arange(64)
    c[0:64, 2, 0:64] = (t[:, None] <= t[None, :]).astype(np.float32)
    c[0:64, 3, 0:64] = np.where(t[None, :] > t[:, None], -1.0e9, 0.0)
    c[0:64, 4, 0:64] = (t[None, :] < t[:, None]).astype(np.float32)
    return c


def prep_m1(inp, xs, L=SEQ):
    maps = []
    w_in = inp["rec_w_in"][0]
    off = np.cumsum([0, 256, 768, 768, 768, 768, 6, 6])
    for core in range(8):
        s, hh = core // 2, core % 2
        hsl = slice(384 * hh, 384 * hh + 384)
        u = w_in[:, off[0] + 128 * hh: off[0] + 128 * hh + 128]
        q = w_in[:, off[1]:off[2]][:, hsl]
        k = w_in[:, off[2]:off[3]][:, hsl]
        v = w_in[:, off[3]:off[4]][:, hsl]
        z = w_in[:, off[4]:off[5]][:, hsl]
        a = w_in[:, off[5]:off[6]][:, 3 * hh:3 * hh + 3]
        b = w_in[:, off[6]:off[7]][:, 3 * hh:3 * hh + 3]
        cvw = inp["dn_conv"][0]
        cw = np.zeros((128, 4, 9), np.float32)
        for part in range(3):
            for hl in range(3):
                c0 = part * 768 + (3 * hh + hl) * 128
                cw[:, :, part * 3 + hl] = cvw[:, c0:c0 + 128].T
        gsl = slice(8 * hh, 8 * hh + 8)
        lre, lim, ldt = inp["s5_lambda_re"][0][gsl], inp["s5_lambda_im"][0][gsl], inp["s5_log_dt"][0][gsl]
        rl = np.zeros((128, 3, 512), np.float32)
        rl[:, 0, :] = lre.reshape(-1)[None, :]
        rl[:, 1, :] = lim.reshape(-1)[None, :]
        rl[:, 2, :] = np.repeat(ldt, 64)[None, :]
        cl = np.zeros((128, 3, 8), np.float32)
        cl[:, 0, :] = np.concatenate([lre.T, lre.T], axis=0)
        cl[:, 1, :] = np.concatenate([lim.T, lim.T], axis=0)
        cl[:, 2, :] = ldt[None, :]
        bre, bim = inp["s5_b_re"][0][gsl], inp["s5_b_im"][0][gsl]
        sb_ = np.zeros((128, 2, 8, 64), np.float32)
        cre, cim = inp["s5_c_re"][0][gsl], inp["s5_c_im"][0][gsl]
        sc_ = np.zeros((128, 2, 8, 128), np.float32)
        for g in range(8):
            sb_[16 * g:16 * g + 16, 0, g, :] = bre[g].T
            sb_[16 * g:16 * g + 16, 1, g, :] = bim[g].T
            sc_[0:64, 0, g, 16 * g:16 * g + 16] = cre[g].T
            sc_[64:128, 0, g, 16 * g:16 * g + 16] = cim[g].T
            sc_[0:64, 1, g, 16 * g:16 * g + 16] = cim[g].T
            sc_[64:128, 1, g, 16 * g:16 * g + 16] = cre[g].T
        maps.append({
            "xT": np.ascontiguousarray(xs[s, 0:L].T),
            "ccol": col(inp["c"][s]),
            "adaw": np.ascontiguousarray(inp["ada_w"][1][:, 0:2048]),
            "adab": col(inp["ada_b"][1][0:2048]),
            "nmix": col(inp["norm_mix"][1]),
            "win": np.ascontiguousarray(np.concatenate([u, q, k, v, z], axis=1)),
            "wab": np.ascontiguousarray(np.concatenate([a, b], axis=1)),
            "convw": cw,
            "dtb": np.tile(inp["dn_dt_bias"][0][3 * hh:3 * hh + 3][None, :], (128, 1)).astype(np.float32),
            "alog": np.tile(inp["dn_a_log"][0][3 * hh:3 * hh + 3][None, :], (128, 1)).astype(np.float32),
            "onorm": np.tile(inp["dn_out_norm"][0][None, :], (128, 1)).astype(np.float32),
            "s5rl": rl, "s5cl": cl, "s5b": sb_.reshape(128, 2, 512), "s5c": sc_,
            "dskip": np.ascontiguousarray(inp["s5_d"][0][128 * hh:128 * hh + 128].reshape(128, 1)),
            "consts": m1_consts(),
            "iota": np.tile(np.arange(NT, dtype=np.float32)[None, :], (128, 1)),
        })
    return maps


def post_m1(results, L=SEQ):
    out = np.zeros((4, 1024, L), ml_dtypes.bfloat16)
    for core in range(8):
        s, hh = core // 2, core % 2
        out[s, 128 * hh:128 * hh + 128] = results[core]["gaT"]
        out[s, 256 + 384 * hh:256 + 384 * hh + 384] = results[core]["ydT"]
    return out


def _run(nc, maps):
    return run_bass_kernel_spmd(nc, maps, core_ids=list(range(8))).results


def kernel(**inputs):
    inp = {k: np.asarray(v) for k, v in inputs.items()}
    yT0 = post_m0(_run(build_m0(), prep_m0(inp)))
    x1 = post_ff(_run(build_ff(False), prep_ff(inp, 0, inp["x"], yT0)))
    yT1 = post_m1(_run(build_m1(), prep_m1(inp, x1)))
    x2 = post_ff(_run(build_ff(True), prep_ff(inp, 1, x1, yT1, glu=True)))
    return np.ascontiguousarray(x2.astype(np.float32))
```
